# Optimizing a Trainium2 kernel written in Bass

```python
import jax, jax.numpy as jnp
from jax import lax
import numpy as np

D_MODEL = 1024
BATCH = 32
SEQ = 2048
DEPTH = 2

D_CONV = D_MODEL
CONV_GROUPS = 16
CONV_WIDTH = 3
D_SGU = D_MODEL
SGU_GROUPS = 8
SGU_GROUP_DIM = D_SGU // SGU_GROUPS
CHUNK = 128
D_FF = 2816
N_BRANCH = 2
EPS = 1e-6

IN_COLS = 3 * D_CONV + 2 * D_SGU + N_BRANCH * D_MODEL

kernel_name = "hybrid_shortconv_sgu_gated_merge"


def rmsnorm(x, g):
    xf = x.astype(jnp.float32)
    y = xf * lax.rsqrt(jnp.mean(xf * xf, axis=-1, keepdims=True) + EPS)
    return (y * g.astype(jnp.float32)).astype(x.dtype)


def layernorm(x, g, b):
    xf = x.astype(jnp.float32)
    mu = jnp.mean(xf, axis=-1, keepdims=True)
    xc = xf - mu
    var = jnp.mean(xc * xc, axis=-1, keepdims=True)
    y = xc * lax.rsqrt(var + EPS)
    return (y * g.astype(jnp.float32) + b.astype(jnp.float32)).astype(x.dtype)


def causal_dwconv3(x, w):
    s = x.shape[1]
    xp = jnp.pad(x, ((0, 0), (CONV_WIDTH - 1, 0), (0, 0)))
    return xp[:, :s] * w[0] + xp[:, 1:s + 1] * w[1] + xp[:, 2:s + 2] * w[2]


def short_conv_mixer(b_gate, c_gate, xin, conv_w):
    return b_gate * causal_dwconv3(c_gate * xin, conv_w)


def spatial_gating_mixer(u, v, ln_g, ln_b, w_s, b_s):
    bsz, s, _ = v.shape
    n_chunks = s // CHUNK
    vn = layernorm(v, ln_g, ln_b).reshape(bsz, n_chunks, CHUNK, SGU_GROUPS, SGU_GROUP_DIM)
    mask = jnp.tril(jnp.ones((CHUNK, CHUNK), dtype=bool))
    w = jnp.where(mask[None], w_s, jnp.zeros((), w_s.dtype))
    mixed = jnp.einsum('gts,bnsgc->bntgc', w, vn)
    mixed = mixed + jnp.swapaxes(b_s, 0, 1)[None, None, :, :, None]
    return u * mixed.reshape(bsz, s, D_SGU)


def conv_gated_mlp(h, w_up, conv_w, w_down):
    up = causal_dwconv3(h @ w_up, conv_w)
    gate, val = jnp.split(up, 2, axis=-1)
    return (jax.nn.silu(gate) * val) @ w_down


def setup_inputs(seed: int = 0) -> dict:
    key = jax.random.key(seed)
    ks = jax.random.split(key, 16)
    f32 = jnp.float32

    def nrm(k, shape, scale):
        return jax.random.normal(k, shape, f32) * scale

    x = jax.random.normal(ks[0], (BATCH, SEQ, D_MODEL), f32)
    mix_norm_g = 1.0 + nrm(ks[1], (DEPTH, D_MODEL), 0.02)
    w_in = nrm(ks[2], (DEPTH, D_MODEL, IN_COLS), D_MODEL ** -0.5)
    conv_a_w = nrm(ks[3], (DEPTH, CONV_WIDTH, D_CONV), CONV_WIDTH ** -0.5)
    ln_v_g = 1.0 + nrm(ks[4], (DEPTH, D_SGU), 0.02)
    ln_v_b = nrm(ks[5], (DEPTH, D_SGU), 0.02)
    w_s = nrm(ks[6], (DEPTH, SGU_GROUPS, CHUNK, CHUNK), CHUNK ** -0.5)
    b_s = 1.0 + nrm(ks[7], (DEPTH, SGU_GROUPS, CHUNK), 0.02)
    w_out = nrm(ks[8], (DEPTH, D_MODEL, D_MODEL), D_MODEL ** -0.5)
    ffn_norm_g = 1.0 + nrm(ks[9], (DEPTH, D_MODEL), 0.02)
    w_up = nrm(ks[10], (DEPTH, D_MODEL, 2 * D_FF), D_MODEL ** -0.5)
    conv_ffn_w = nrm(ks[11], (DEPTH, CONV_WIDTH, 2 * D_FF), CONV_WIDTH ** -0.5)
    w_down = nrm(ks[12], (DEPTH, D_FF, D_MODEL), D_FF ** -0.5)
    final_norm_g = 1.0 + nrm(ks[13], (D_MODEL,), 0.02)
    return {"x": x, "mix_norm_g": mix_norm_g, "w_in": w_in, "conv_a_w": conv_a_w,
            "ln_v_g": ln_v_g, "ln_v_b": ln_v_b, "w_s": w_s, "b_s": b_s,
            "w_out": w_out, "ffn_norm_g": ffn_norm_g, "w_up": w_up,
            "conv_ffn_w": conv_ffn_w, "w_down": w_down, "final_norm_g": final_norm_g}


def reference(x, mix_norm_g, w_in, conv_a_w, ln_v_g, ln_v_b, w_s, b_s, w_out,
              ffn_norm_g, w_up, conv_ffn_w, w_down, final_norm_g):
    split_pts = [D_CONV, 2 * D_CONV, 3 * D_CONV, 3 * D_CONV + D_SGU,
                 3 * D_CONV + 2 * D_SGU, 3 * D_CONV + 2 * D_SGU + D_MODEL]
    for l in range(DEPTH):
        h = rmsnorm(x, mix_norm_g[l])
        proj = h @ w_in[l]
        b_gate, c_gate, xin, u, v, g_a, g_b = jnp.split(proj, split_pts, axis=-1)
        y_a = short_conv_mixer(b_gate, c_gate, xin, conv_a_w[l])
        y_b = spatial_gating_mixer(u, v, ln_v_g[l], ln_v_b[l], w_s[l], b_s[l])
        merged = jax.nn.sigmoid(g_a) * y_a + jax.nn.sigmoid(g_b) * y_b
        x = x + merged @ w_out[l]
        h = rmsnorm(x, ffn_norm_g[l])
        x = x + conv_gated_mlp(h, w_up[l], conv_ffn_w[l], w_down[l])
    return rmsnorm(x, final_norm_g)
```

```python
import contextlib
import numpy as np
import concourse.bass as bass
import concourse.mybir as mybir
from concourse.bass_utils import run_bass_kernel_spmd

F32 = mybir.dt.float32
BF16 = mybir.dt.bfloat16
AF = mybir.ActivationFunctionType
ALU = mybir.AluOpType

D = 1024
S = 2048
DEPTH = 2
NCORES = 8
BATCH = 32
G = 1024
KT = 8
DFF = 2816
NJ = 22
EPS = 1e-6
SLOT = 2048
NSLOT_D = 65
NSLOT_L = 76
LW = NSLOT_D * SLOT
NS = 8
NPL = 180
NPAR = DEPTH * NPL + 8
PE_PAD = 0

C_B, C_C, C_XI, C_U, C_V, C_GA, C_GB = 0, 1024, 2048, 3072, 4096, 5120, 6144


def _tiles(w, cols):
    k = w.shape[0] // 128
    a = w.reshape(k, 128, w.shape[1]).transpose(1, 0, 2)
    return a[:, :, cols]


def pack_layer(w_in, w_out, w_up, w_down):
    parts = []
    a_in = w_in.reshape(KT, 128, -1).transpose(1, 0, 2)
    parts.append(a_in[:, :, C_V:C_V + 1024].reshape(128, -1))
    for f in range(8):
        for c0 in (C_XI, C_C, C_B, C_GA, C_U, C_GB):
            parts.append(a_in[:, :, c0 + f * 128:c0 + (f + 1) * 128].reshape(128, -1))
    a_o = w_out.reshape(KT, 128, -1).transpose(1, 0, 2)
    for m in range(8):
        parts.append(a_o[:, :, m * 128:(m + 1) * 128].reshape(128, -1))
    a_u = w_up.reshape(KT, 128, -1).transpose(1, 0, 2)
    for j in range(NJ):
        parts.append(a_u[:, :, j * 128:(j + 1) * 128].reshape(128, -1))
        parts.append(a_u[:, :, DFF + j * 128:DFF + (j + 1) * 128].reshape(128, -1))
    a_d = w_down.reshape(NJ, 128, -1).transpose(1, 0, 2)
    for m in range(8):
        parts.append(a_d[:, :, m * 128:(m + 1) * 128].reshape(128, -1))
    out = np.concatenate(parts, axis=1)
    assert out.shape == (128, LW), out.shape
    return out


def pack_params(mix_norm_g, ln_v_g, ffn_norm_g, conv_a_w, conv_ffn_w, final_norm_g):
    par = np.zeros((128, NPAR), np.float32)
    for l in range(DEPTH):
        b = l * NPL
        par[:, b + 0:b + 8] = mix_norm_g[l].reshape(8, 128).T
        par[:, b + 8:b + 16] = ln_v_g[l].reshape(8, 128).T
        par[:, b + 16:b + 24] = ffn_norm_g[l].reshape(8, 128).T
        par[:, b + 24:b + 48] = conv_a_w[l].reshape(3, 8, 128).transpose(2, 0, 1).reshape(128, 24)
        par[:, b + 48:b + 180] = conv_ffn_w[l].reshape(3, 44, 128).transpose(2, 0, 1).reshape(128, 132)
    par[:, DEPTH * NPL:DEPTH * NPL + 8] = final_norm_g.reshape(8, 128).T
    return par


class Op:
    __slots__ = ("eng", "fn", "deps", "ref", "count", "dma", "dma_sem", "dma_val")

    def __init__(self, eng, fn):
        self.eng = eng
        self.fn = fn
        self.deps = []
        self.ref = False
        self.count = 0
        self.dma = False
        self.dma_sem = None
        self.dma_val = 0


class Buf:
    __slots__ = ("w", "r")

    def __init__(self):
        self.w = None
        self.r = {}


class Prog:
    ENGS = ("pe", "act", "dve", "pool", "sp")

    def __init__(self):
        self.q = {e: [] for e in self.ENGS}
        self.dma_cnt = {}
        self.nuid = 0

    def add(self, eng, fn, reads=(), writes=(), dma_sem=None):
        op = Op(eng, fn)
        is_dma = dma_sem is not None
        deps = {}

        def dep(d, raw):
            if d is None:
                return
            if (not d.dma) and (not is_dma) and d.eng == eng and not raw:
                return
            deps[id(d)] = d

        for b in reads:
            dep(b.w, True)
        for b in writes:
            dep(b.w, False)
            for r in b.r.values():
                dep(r, False)
        op.deps = list(deps.values())
        for d in op.deps:
            d.ref = True
        if is_dma:
            op.dma = True
            op.dma_sem = dma_sem
            self.dma_cnt[id(dma_sem)] = self.dma_cnt.get(id(dma_sem), 0) + 16
            op.dma_val = self.dma_cnt[id(dma_sem)]
        for b in reads:
            if is_dma:
                self.nuid += 1
                b.r[("dma", self.nuid)] = op
            else:
                b.r[eng] = op
        for b in writes:
            b.w = op
            b.r = {}
        self.q[eng].append(op)
        return op

    def assign(self):
        for e in self.ENGS:
            cnt = 0
            for op in self.q[e]:
                if op.ref and not op.dma:
                    cnt += 1
                    op.count = cnt

    def emit_engine(self, eng, h, sems, final_waits=()):
        waited = {}
        nwait = 0
        for op in self.q[eng]:
            for d in op.deps:
                if d.dma:
                    sem, val = d.dma_sem, d.dma_val
                else:
                    sem, val = sems[d.eng], d.count
                if waited.get(id(sem), 0) < val:
                    h.wait_ge(sem, val)
                    waited[id(sem)] = val
                    nwait += 1
            ins = op.fn(h)
            if op.dma:
                ins.then_inc(op.dma_sem, 16)
            elif op.ref:
                ins.then_inc(sems[eng], 1)
        for sem, val in final_waits:
            h.wait_ge(sem, val)
        return nwait


def build(nseq):
    nc = bass.Bass("TRN2", target_bir_lowering=False)
    xT = nc.dram_tensor("xT", [nseq, D, S], F32, kind="ExternalInput").ap()
    ws = nc.dram_tensor("ws", [DEPTH, 128, LW], F32, kind="ExternalInput").ap()
    par_d = nc.dram_tensor("par", [128, NPAR], F32, kind="ExternalInput").ap()
    lnb_d = nc.dram_tensor("lnb", [DEPTH, D], F32, kind="ExternalInput").ap()
    bs_d = nc.dram_tensor("bsd", [DEPTH, D], F32, kind="ExternalInput").ap()
    wst_d = nc.dram_tensor("wst", [DEPTH, 128, 8, 128], F32, kind="ExternalInput").ap()
    oT = nc.dram_tensor("oT", [nseq, D, S], F32, kind="ExternalOutput").ap()

    P = Prog()
    ngroups = 2 * nseq

    with contextlib.ExitStack() as es:
        def sb(name, shape, dt):
            return es.enter_context(nc.sbuf_tensor(name, shape, dt))

        xs = sb("xs", [128, 2, 8, G], F32)
        hs = sb("hs", [128, 8, G], BF16)
        big = sb("big", [128, NJ * G], BF16)
        ring = sb("ring", [128, NS, SLOT], BF16)
        tmp = sb("tmp", [128, 6, G + 2], F32)
        sq = sb("sq", [128, 2, G], BF16)
        rstd = sb("rstd", [128, G], F32)
        par = sb("par_sb", [128, NPAR], F32)
        WT = sb("WT", [128, DEPTH, 8, 128], BF16)
        bias = sb("bias", [128, DEPTH, 8 * 128], F32)
        ones = sb("ones", [128, 128], BF16)
        epsb = sb("epsb", [128, 1], F32)
        hm = sb("hm", [128, DEPTH, 8, 2], F32)
        hf_ = sb("hf", [128, DEPTH, 44, 2], F32)
        st = sb("st", [128, 2, 2, 6], F32)
        mv = sb("mv", [128, 2, 8], F32)
        dmy = sb("dmy", [128, 1], F32)
        bb = sb("bb", [128, 44, 2], F32)
        btt = sb("btt", [128, 44], F32)
        ps = es.enter_context(nc.psum_tensor("ps", [128, 4, G], F32))

        sems = {e: es.enter_context(nc.semaphore("s_" + e)) for e in Prog.ENGS}
        rsem = [es.enter_context(nc.semaphore("rs%d" % i)) for i in range(NS)]
        xsem = [[es.enter_context(nc.semaphore("xs%d_%d" % (b_, i))) for i in range(8)] for b_ in range(2)]
        osem = [es.enter_context(nc.semaphore("os%d" % i)) for i in range(8)]
        csem = es.enter_context(nc.semaphore("cs"))

        XB = [[Buf() for _ in range(8)] for _ in range(2)]
        HB = [Buf() for _ in range(8)]
        AB = [Buf() for _ in range(NJ)]
        RB = [Buf() for _ in range(NS)]
        TB = [Buf() for _ in range(6)]
        SQB = [Buf(), Buf()]
        RSB = Buf()
        PSB = [Buf() for _ in range(4)]
        CONST = Buf()
        HMB = [[Buf() for _ in range(8)] for _ in range(DEPTH)]
        HFB = [Buf() for _ in range(DEPTH)]
        BBB = Buf()
        STB = [Buf(), Buf()]
        DMYB = Buf()

        def act_bf(j):
            return big[:, j * G:(j + 1) * G]

        def out_f32(f):
            return big[:, f * 2 * G:(f + 1) * 2 * G].bitcast(F32)

        state = {"next": 0, "held": set()}

        def alloc_slot(hold=False):
            while True:
                s_ = state["next"] % 4
                state["next"] += 1
                if s_ not in state["held"]:
                    break
            if hold:
                state["held"].add(s_)
            return s_

        def unhold(s_):
            state["held"].discard(s_)

        total_slots = ngroups * DEPTH * NSLOT_L
        wstate = {"issued": 0, "retired": -1}

        def issue_dma():
            q = wstate["issued"]
            lay = (q // NSLOT_L) % DEPTH
            ql = q % NSLOT_L
            if ql >= NSLOT_D:
                ql = ql - NSLOT_D + 54
            r = q % NS
            P.add("pool",
                  lambda e, r=r, lay=lay, ql=ql: e.dma_start(
                      out=ring[:, r, :], in_=ws[lay, :, ql * SLOT:(ql + 1) * SLOT]),
                  writes=[RB[r]], dma_sem=rsem[r])
            wstate["issued"] += 1

        def retire_upto(gq):
            while wstate["retired"] < gq:
                wstate["retired"] += 1
                if wstate["issued"] < total_slots:
                    issue_dma()

        def wtile(gl, ti, ntiles=1, pinned=False):
            gq = gl * NSLOT_L + ti // 16
            if not pinned:
                retire_upto(gq - 1)
            assert wstate["issued"] > gq, (wstate, gq)
            r = gq % NS
            off = (ti % 16) * 128
            return ring[:, r, off:off + 128 * ntiles], RB[r]

        P.add("sp", lambda e: e.dma_start(out=par[:], in_=par_d), writes=[CONST], dma_sem=csem)
        for l in range(DEPTH):
            P.add("sp", lambda e, l=l: e.dma_start(out=tmp[:, l, 0:D], in_=lnb_d[l].partition_broadcast(128)),
                  writes=[TB[l]], dma_sem=csem)
            P.add("sp", lambda e, l=l: e.dma_start(out=tmp[:, 2 + l, 0:D], in_=bs_d[l].partition_broadcast(128)),
                  writes=[TB[2 + l]], dma_sem=csem)
            P.add("sp", lambda e, l=l: e.dma_start(
                out=tmp[:, 4 + l, 0:D].rearrange("p (g t) -> p g t", g=8), in_=wst_d[l]),
                writes=[TB[4 + l]], dma_sem=csem)
        for op in P.q["sp"]:
            if op.dma and op.dma_sem is csem:
                op.dma_val = P.dma_cnt[id(csem)]
        betabf = rstd[:].bitcast(BF16).rearrange("p (l d) -> p l d", l=DEPTH)
        for _ in range(min(NS, total_slots)):
            issue_dma()
        P.add("dve", lambda e: e.memset(ones[:], 1.0), writes=[CONST])
        P.add("dve", lambda e: e.memset(epsb[:], EPS), writes=[CONST])
        for l in range(DEPTH):
            P.add("dve", lambda e, l=l: e.tensor_copy(out=betabf[:, l, :], in_=tmp[:, l, 0:D]),
                  reads=[TB[l]], writes=[CONST, RSB])
            P.add("pool", lambda e, l=l: e.affine_select(
                out=WT[:, l, :, :], in_=tmp[:, 4 + l, 0:D].rearrange("p (g t) -> p g t", g=8),
                pattern=[[0, 8], [1, 128]], compare_op=ALU.is_ge, fill=0.0, base=0,
                channel_multiplier=-1), reads=[TB[4 + l]], writes=[CONST])
        for l in range(DEPTH):
            s_ = alloc_slot()
            for g in range(8):
                P.add("pe", lambda e, l=l, g=g, s_=s_: e.matmul(
                    ps[:, s_, g * 128:(g + 1) * 128], betabf[:, l, g * 128:(g + 1) * 128],
                    WT[:, l, g, :], start=True, stop=True, skip_group_check=True),
                    reads=[CONST, RSB], writes=[PSB[s_]])
            P.add("dve", lambda e, l=l, s_=s_: e.tensor_tensor(
                out=bias[:, l, :], in0=ps[:, s_, :], in1=tmp[:, 2 + l, 0:D], op=ALU.add),
                reads=[PSB[s_], TB[2 + l]], writes=[CONST])

        sqv = sq[:].rearrange("p a g -> p (a g)").rearrange("p (r c) -> p r c", r=4)
        SQB4 = [Buf() for _ in range(4)]
        RSB2 = [RSB, Buf()]
        PNB = [[Buf(), Buf()] for _ in range(4)]
        HB2 = [[Buf(), Buf()] for _ in range(8)]
        sqrot = {"i": 0}

        def H(tt):
            return slice(tt * 512, (tt + 1) * 512)

        def norm_acc_tt(xb, f, tt, pn, first, last, defer=None):
            r = sqrot["i"] % 4
            sqrot["i"] += 1
            P.add("act", lambda e, f=f, r=r, xb=xb, tt=tt: e.activation(
                out=sqv[:, r, :], in_=xs[:, xb, f, H(tt)], func=AF.Square),
                reads=[XB[xb][f]], writes=[SQB4[r]])

            def pe_part(r=r, tt=tt, pn=pn, first=first, last=last):
                P.add("pe", lambda e: e.matmul(
                    ps[:, pn, H(tt)], ones[:], sqv[:, r, :], start=first, stop=last),
                    reads=[SQB4[r], CONST], writes=([PSB[pn]] if first else []) + [PNB[pn][tt]])
            if defer is None:
                pe_part()
            else:
                defer.append(pe_part)

        def norm_acc(xb, f, pn, first, last):
            for tt in range(2):
                norm_acc_tt(xb, f, tt, pn, first, last)

        def norm_rstd_tt(pn, tt):
            P.add("act", lambda e, pn=pn, tt=tt: e.activation(
                out=rstd[:, H(tt)], in_=ps[:, pn, H(tt)], func=AF.Ln, bias=epsb[:, 0:1], scale=1.0 / D),
                reads=[PNB[pn][tt], PSB[pn], CONST], writes=[RSB2[tt]])
            P.add("act", lambda e, tt=tt: e.activation(
                out=rstd[:, H(tt)], in_=rstd[:, H(tt)], func=AF.Exp, scale=-0.5),
                reads=[RSB2[tt]], writes=[RSB2[tt]])

        def norm_apply_tt(xb, tt, gcol, dst):
            for k in range(8):
                out_ap, wb = dst(k, tt)
                P.add("dve", lambda e, k=k, out_ap=out_ap, xb=xb, tt=tt: e.scalar_tensor_tensor(
                    out=out_ap, in0=xs[:, xb, k, H(tt)], scalar=par[:, gcol + k:gcol + k + 1], in1=rstd[:, H(tt)],
                    op0=ALU.mult, op1=ALU.mult),
                    reads=[XB[xb][k], RSB2[tt], CONST], writes=wb)

        def norm_finish_tt(xb, pn, tt, gcol, dst):
            norm_rstd_tt(pn, tt)
            norm_apply_tt(xb, tt, gcol, dst)

        def norm_finish(xb, pn, gcol, dst):
            for tt in range(2):
                norm_finish_tt(xb, pn, tt, gcol, dst)
            unhold(pn)

        def h_dst(k, tt):
            return hs[:, k, H(tt)], [HB2[k][tt]]

        def load_x(g_):
            s_, h_ = divmod(g_, 2)
            b_ = g_ % 2
            for f in range(8):
                P.add("sp", lambda e, f=f, s_=s_, h_=h_, b_=b_: e.dma_start(
                    out=xs[:, b_, f, :], in_=xT[s_, f * 128:(f + 1) * 128, h_ * G:(h_ + 1) * G]),
                    writes=[XB[b_][f]], dma_sem=xsem[b_][f])

        def proj(gl, ti0, rhs_of, rhs_bufs_of, nk, s_, tts=(0, 1), pinned=False, obank=None):
            for k in range(nk):
                w_ap, wb = wtile(gl, ti0 + k, pinned=pinned)
                for tt in tts:
                    ob = tt if obank is None else obank
                    P.add("pe", lambda e, w_ap=w_ap, k=k, tt=tt, s_=s_, ob=ob: e.matmul(
                        ps[:, s_, H(ob)], w_ap, rhs_of(k, tt), start=(k == 0), stop=(k == nk - 1)),
                        reads=[wb] + rhs_bufs_of(k, tt), writes=[PSB[s_]])

        def hk(k, tt):
            return hs[:, k, H(tt)]

        def hkb(k, tt):
            return [HB2[k][tt]]

        def mk(k, tt):
            return act_bf(8 + k)[:, H(tt)]

        def mkb(k, tt):
            return [AB[8 + k]]

        def ak(k, tt):
            return act_bf(k)[:, H(tt)]

        def akb(k, tt):
            return [AB[k]]

        def resid_phase(gl, xb, ti_of, nk, rhs_of, rhs_bufs_of, pn, pinned, after_tt, extra_after_prod):
            P.add("act", lambda e: e.activation(out=dmy[:], in_=epsb[:], func=AF.Exp), reads=[CONST], writes=[DMYB])
            deferred = {0: [], 1: []}

            def finish(tt):
                for fn in deferred[tt]:
                    fn()
                deferred[tt] = []
                after_tt(tt)

            for tt in range(2):
                pend = []
                for mp in range(4):
                    s_o = alloc_slot()
                    if pinned:
                        for k in range(nk):
                            for mi in range(2):
                                w_ap, wb = wtile(gl, ti_of(2 * mp + mi, tt) + k, pinned=True)
                                P.add("pe", lambda e, w_ap=w_ap, k=k, tt=tt, s_=s_o, mi=mi: e.matmul(
                                    ps[:, s_, H(mi)], w_ap, rhs_of(k, tt), start=(k == 0), stop=(k == nk - 1)),
                                    reads=[wb] + rhs_bufs_of(k, tt), writes=[PSB[s_o]])
                    else:
                        for mi in range(2):
                            m = 2 * mp + mi
                            proj(gl, ti_of(m, tt), rhs_of, rhs_bufs_of, nk, s_o, tts=(tt,), pinned=pinned, obank=mi)
                    P.add("dve", lambda e, mp=mp, tt=tt, s_=s_o, xb=xb: e.tensor_tensor(
                        out=xs[:, xb, 2 * mp:2 * mp + 2, H(tt)],
                        in0=ps[:, s_, :].rearrange("p (a c) -> p a c", a=2),
                        in1=xs[:, xb, 2 * mp:2 * mp + 2, H(tt)], op=ALU.add),
                        reads=[PSB[s_o], XB[xb][2 * mp], XB[xb][2 * mp + 1]],
                        writes=[XB[xb][2 * mp], XB[xb][2 * mp + 1]])
                    if tt == 1 and mp == 0:
                        finish(0)
                    extra_after_prod(tt, mp)
                    pend.append(mp)
                    if len(pend) > 1:
                        q_ = pend.pop(0)
                        for m in (2 * q_, 2 * q_ + 1):
                            norm_acc_tt(xb, m, tt, pn, m == 0, False)
                for q_ in pend:
                    for m in (2 * q_, 2 * q_ + 1):
                        norm_acc_tt(xb, m, tt, pn, m == 0, m == 7, defer=deferred[tt])
            return lambda: finish(1)

        def emit_final_act(fin):
            xb_, pn_, s_, t0_, cb_ = fin
            cb_()
            for tt in range(2):
                norm_rstd_tt(pn_, tt)
            unhold(pn_)

        def emit_final_dve(fin):
            xb_, pn_, s_, t0_, cb_ = fin

            def out_dst(k, tt, xb_=xb_):
                return xs[:, xb_, k, H(tt)], [XB[xb_][k]]
            for tt in range(2):
                norm_apply_tt(xb_, tt, DEPTH * NPL, out_dst)
            for f in range(8):
                P.add("sp", lambda e, f=f, s_=s_, t0_=t0_, xb_=xb_: e.dma_start(
                    out=oT[s_, f * 128:(f + 1) * 128, t0_:t0_ + G], in_=xs[:, xb_, f, :]),
                    reads=[XB[xb_][f]], dma_sem=osem[f])

        load_x(0)
        pn0 = alloc_slot(hold=True)
        for f in range(8):
            norm_acc(0, f, pn0, f == 0, f == 7)
        norm_finish(0, pn0, 0, h_dst)
        pending_final = None
        pending_cb = None
        for gi in range(ngroups):
            s, hfi = divmod(gi, 2)
            t0 = hfi * G
            xb = gi % 2

            for l in range(DEPTH):
                gl = gi * DEPTH + l
                pb = l * NPL
                retire_upto(gl * NSLOT_L - 1)
                for n in range(8):
                    s_ = alloc_slot()
                    for k in range(8):
                        for half in range(2):
                            w_ap, wb = wtile(gl, k * 8 + half * 4, 4, pinned=True)
                            P.add("pe", lambda e, w_ap=w_ap, k=k, n=n, half=half, s_=s_: e.matmul(
                                ps[:, s_, H(half)], hs[:, k, n * 128:(n + 1) * 128], w_ap,
                                start=(k == 0), stop=(k == 7)),
                                reads=[wb, HB2[k][n // 4]], writes=[PSB[s_]])
                    if n == 0:
                        if pending_cb is not None:
                            pending_cb()
                            pending_cb = None
                        if l == 0 and pending_final is not None:
                            emit_final_act(pending_final)
                    r = n % 2
                    for half in range(2):
                        P.add("dve", lambda e, r=r, half=half, s_=s_: e.bn_stats(
                            out=st[:, r, half, :], in_=ps[:, s_, H(half)]),
                            reads=[PSB[s_]], writes=[STB[r]])
                    P.add("dve", lambda e, r=r: e.bn_aggr(
                        out=mv[:, r, 0:2], in_=st[:, r, :, :].rearrange("p a b -> p (a b)")),
                        reads=[STB[r]], writes=[STB[r]])
                    P.add("act", lambda e, r=r: e.activation(out=mv[:, r, 2:3], in_=mv[:, r, 1:2], func=AF.Ln,
                                                            bias=epsb[:, 0:1], scale=1.0),
                          reads=[STB[r], CONST], writes=[STB[r]])
                    P.add("act", lambda e, r=r: e.activation(out=mv[:, r, 3:4], in_=mv[:, r, 2:3], func=AF.Exp,
                                                            scale=-0.5),
                          reads=[STB[r]], writes=[STB[r]])
                    P.add("dve", lambda e, r=r: e.tensor_scalar(
                        out=mv[:, r, 4:5], in0=mv[:, r, 0:1], scalar1=-1.0, scalar2=mv[:, r, 3:4],
                        op0=ALU.mult, op1=ALU.mult), reads=[STB[r]], writes=[STB[r]])
                    P.add("act", lambda e, r=r, n=n, s_=s_: e.activation(
                        out=act_bf(n), in_=ps[:, s_, :], func=AF.Identity,
                        bias=mv[:, r, 4:5], scale=mv[:, r, 3:4]),
                        reads=[PSB[s_], STB[r]], writes=[AB[n]])

                if l == 0 and pending_final is not None:
                    emit_final_dve(pending_final)
                    pending_final = None
                if l == 0 and gi + 1 < ngroups:
                    load_x(gi + 1)

                XI, SA, SG = 0, 4, 4
                for f in range(8):
                    cxi = 1
                    tai = 2 + (f % 2)
                    m1i = 5
                    tib = 64 + f * 48
                    cw = [par[:, pb + 24 + j * 8 + f:pb + 24 + j * 8 + f + 1] for j in range(3)]
                    lng = par[:, pb + 8 + f:pb + 8 + f + 1]
                    s_xi = alloc_slot()
                    proj(gl, tib + 0, hk, hkb, 8, s_xi)
                    P.add("act", lambda e, s_=s_xi: e.activation(out=tmp[:, XI, 0:G], in_=ps[:, s_, :], func=AF.Copy),
                          reads=[PSB[s_xi]], writes=[TB[XI]])
                    s_c = alloc_slot()
                    proj(gl, tib + 8, hk, hkb, 8, s_c)
                    if hfi == 0:
                        P.add("dve", lambda e, cxi=cxi: e.memset(tmp[:, cxi, 0:2], 0.0), writes=[TB[cxi]])
                    else:
                        P.add("dve", lambda e, cxi=cxi, l=l, f=f: e.tensor_copy(out=tmp[:, cxi, 0:2], in_=hm[:, l, f, :]),
                              reads=[HMB[l][f]], writes=[TB[cxi]])
                    P.add("dve", lambda e, cxi=cxi, s_=s_c: e.tensor_tensor(
                        out=tmp[:, cxi, 2:G + 2], in0=ps[:, s_, :], in1=tmp[:, XI, 0:G], op=ALU.mult),
                        reads=[PSB[s_c], TB[XI]], writes=[TB[cxi]])
                    if hfi == 0:
                        P.add("dve", lambda e, cxi=cxi, l=l, f=f: e.tensor_copy(out=hm[:, l, f, :], in_=tmp[:, cxi, G:G + 2]),
                              reads=[TB[cxi]], writes=[HMB[l][f]])
                    P.add("act", lambda e, cxi=cxi, tai=tai, c2=cw[2]: e.activation(
                        out=tmp[:, tai, 0:G], in_=tmp[:, cxi, 2:G + 2], func=AF.Copy, scale=c2),
                        reads=[TB[cxi], CONST], writes=[TB[tai]])
                    P.add("dve", lambda e, cxi=cxi, tai=tai, c1=cw[1]: e.scalar_tensor_tensor(
                        out=tmp[:, tai, 0:G], in0=tmp[:, cxi, 1:G + 1], scalar=c1, in1=tmp[:, tai, 0:G],
                        op0=ALU.mult, op1=ALU.add), reads=[TB[cxi], TB[tai], CONST], writes=[TB[tai]])
                    P.add("dve", lambda e, cxi=cxi, tai=tai, c0=cw[0]: e.scalar_tensor_tensor(
                        out=tmp[:, tai, 0:G], in0=tmp[:, cxi, 0:G], scalar=c0, in1=tmp[:, tai, 0:G],
                        op0=ALU.mult, op1=ALU.add), reads=[TB[cxi], TB[tai], CONST], writes=[TB[tai]])
                    s_b = alloc_slot()
                    proj(gl, tib + 16, hk, hkb, 8, s_b)
                    P.add("dve", lambda e, tai=tai, s_=s_b: e.tensor_tensor(
                        out=tmp[:, tai, 0:G], in0=ps[:, s_, :], in1=tmp[:, tai, 0:G], op=ALU.mult),
                        reads=[PSB[s_b], TB[tai]], writes=[TB[tai]])
                    s_ga = alloc_slot()
                    proj(gl, tib + 24, hk, hkb, 8, s_ga)
                    P.add("act", lambda e, s_=s_ga: e.activation(out=tmp[:, SA, 0:G], in_=ps[:, s_, :], func=AF.Sigmoid),
                          reads=[PSB[s_ga]], writes=[TB[SA]])
                    P.add("dve", lambda e, tai=tai: e.tensor_tensor(
                        out=tmp[:, tai, 0:G], in0=tmp[:, tai, 0:G], in1=tmp[:, SA, 0:G], op=ALU.mult),
                        reads=[TB[tai], TB[SA]], writes=[TB[tai]])
                    s_sg = alloc_slot()
                    for n in range(8):
                        P.add("pe", lambda e, n=n, f=f, l=l, s_=s_sg: e.matmul(
                            ps[:, s_, n * 128:(n + 1) * 128], act_bf(n)[:, f * 128:(f + 1) * 128], WT[:, l, f, :],
                            start=True, stop=True, skip_group_check=True),
                            reads=[AB[n], CONST], writes=[PSB[s_sg]])
                    P.add("dve", lambda e, m1i=m1i, lng=lng, l=l, f=f, s_=s_sg: e.scalar_tensor_tensor(
                        out=tmp[:, m1i, 0:G].rearrange("p (n t) -> p n t", n=8),
                        in0=ps[:, s_, :].rearrange("p (n t) -> p n t", n=8), scalar=lng,
                        in1=bias[:, l, f * 128:(f + 1) * 128].unsqueeze(1).broadcast_to([128, 8, 128]),
                        op0=ALU.mult, op1=ALU.add), reads=[PSB[s_sg], CONST], writes=[TB[m1i]])
                    s_u = alloc_slot()
                    proj(gl, tib + 32, hk, hkb, 8, s_u)
                    P.add("dve", lambda e, m1i=m1i, s_=s_u: e.tensor_tensor(
                        out=tmp[:, m1i, 0:G], in0=ps[:, s_, :], in1=tmp[:, m1i, 0:G], op=ALU.mult),
                        reads=[PSB[s_u], TB[m1i]], writes=[TB[m1i]])
                    s_gb = alloc_slot()
                    proj(gl, tib + 40, hk, hkb, 8, s_gb)
                    P.add("act", lambda e, s_=s_gb: e.activation(out=tmp[:, SG, 0:G], in_=ps[:, s_, :], func=AF.Sigmoid),
                          reads=[PSB[s_gb]], writes=[TB[SG]])
                    P.add("dve", lambda e, m1i=m1i: e.tensor_tensor(
                        out=tmp[:, m1i, 0:G], in0=tmp[:, m1i, 0:G], in1=tmp[:, SG, 0:G], op=ALU.mult),
                        reads=[TB[m1i], TB[SG]], writes=[TB[m1i]])
                    P.add("dve", lambda e, m1i=m1i, tai=tai, f=f: e.tensor_tensor(
                        out=act_bf(8 + f), in0=tmp[:, tai, 0:G], in1=tmp[:, m1i, 0:G], op=ALU.add),
                        reads=[TB[tai], TB[m1i]], writes=[AB[8 + f]])

                retire_upto(gl * NSLOT_L + 27)
                pn = alloc_slot(hold=True)

                def after_m3(tt, xb=xb, pb=pb, pn=pn):
                    norm_finish_tt(xb, pn, tt, pb + 16, h_dst)
                    if tt == 1:
                        unhold(pn)

                m3_cb = resid_phase(gl, xb, lambda m, tt: 448 + m * 8, 8, mk, mkb, pn, True, after_m3,
                                    lambda tt, mp: None)
                retire_upto(gl * NSLOT_L + 31)

                if hfi == 1:
                    W0 = par[:, pb + 48:pb + 92]
                    W1 = par[:, pb + 92:pb + 136]
                    P.add("dve", lambda e, l=l, W0=W0: e.tensor_tensor(out=bb[:, :, 1], in0=hf_[:, l, :, 1], in1=W0, op=ALU.mult),
                          reads=[HFB[l], CONST], writes=[BBB])
                    P.add("dve", lambda e, l=l, W0=W0: e.tensor_tensor(out=bb[:, :, 0], in0=hf_[:, l, :, 0], in1=W0, op=ALU.mult),
                          reads=[HFB[l], CONST], writes=[BBB])
                    P.add("dve", lambda e, l=l, W1=W1: e.tensor_tensor(out=btt[:], in0=hf_[:, l, :, 1], in1=W1, op=ALU.mult),
                          reads=[HFB[l], CONST], writes=[BBB])
                    P.add("dve", lambda e: e.tensor_tensor(out=bb[:, :, 0], in0=bb[:, :, 0], in1=btt[:], op=ALU.add),
                          reads=[BBB], writes=[BBB])

                def f1_evac(j, s_g, s_v):
                    cgi = j % 2
                    cvi = 2 + (j % 2)
                    for jj, s_p, ci in ((j, s_g, cgi), (22 + j, s_v, cvi)):
                        fw = [par[:, pb + 48 + t * 44 + jj:pb + 48 + t * 44 + jj + 1] for t in range(3)]
                        P.add("act", lambda e, ci=ci, s_=s_p, w2=fw[2]: e.activation(
                            out=tmp[:, ci, 0:G], in_=ps[:, s_, :], func=AF.Copy, scale=w2),
                            reads=[PSB[s_p], CONST], writes=[TB[ci]])
                        P.add("dve", lambda e, ci=ci, s_=s_p, w1=fw[1]: e.scalar_tensor_tensor(
                            out=tmp[:, ci, 1:G], in0=ps[:, s_, 0:G - 1], scalar=w1, in1=tmp[:, ci, 1:G],
                            op0=ALU.mult, op1=ALU.add), reads=[PSB[s_p], TB[ci], CONST], writes=[TB[ci]])
                        P.add("dve", lambda e, ci=ci, s_=s_p, w0=fw[0]: e.scalar_tensor_tensor(
                            out=tmp[:, ci, 2:G], in0=ps[:, s_, 0:G - 2], scalar=w0, in1=tmp[:, ci, 2:G],
                            op0=ALU.mult, op1=ALU.add), reads=[PSB[s_p], TB[ci], CONST], writes=[TB[ci]])
                        if hfi == 0:
                            P.add("act", lambda e, l=l, jj=jj, s_=s_p: e.activation(
                                out=hf_[:, l, jj, :], in_=ps[:, s_, G - 2:G], func=AF.Copy),
                                reads=[PSB[s_p]], writes=[HFB[l]])
                        else:
                            P.add("dve", lambda e, ci=ci, jj=jj: e.tensor_tensor(
                                out=tmp[:, ci, 0:2], in0=tmp[:, ci, 0:2], in1=bb[:, jj, :], op=ALU.add),
                                reads=[BBB, TB[ci]], writes=[TB[ci]])
                    P.add("act", lambda e, cgi=cgi: e.activation(out=tmp[:, cgi, 0:G], in_=tmp[:, cgi, 0:G], func=AF.Silu),
                          reads=[TB[cgi]], writes=[TB[cgi]])
                    P.add("pool", lambda e, cgi=cgi, cvi=cvi, j=j: e.tensor_tensor(
                        out=act_bf(j), in0=tmp[:, cgi, 0:G], in1=tmp[:, cvi, 0:G], op=ALU.mult),
                        reads=[TB[cgi], TB[cvi]], writes=[AB[j]])

                slots01 = [[None, None], [None, None]]
                for tt in range(2):
                    for j in range(2):
                        for w_ in range(2):
                            if tt == 0:
                                slots01[j][w_] = alloc_slot()
                            proj(gl, 512 + j * 16 + 8 * w_, hk, hkb, 8, slots01[j][w_], tts=(tt,), pinned=True)
                            if tt == 0 and j == 0 and w_ == 0:
                                m3_cb()
                for j in range(2):
                    f1_evac(j, *slots01[j])
                for j in range(2, NJ):
                    s_g = alloc_slot()
                    proj(gl, 512 + j * 16, hk, hkb, 8, s_g)
                    s_v = alloc_slot()
                    proj(gl, 512 + j * 16 + 8, hk, hkb, 8, s_v)
                    f1_evac(j, s_g, s_v)

                pn = alloc_slot(hold=True)
                last_layer = (l == DEPTH - 1)
                do_next = last_layer and (gi + 1 < ngroups)
                nxb = (gi + 1) % 2
                if do_next:
                    pn_next = alloc_slot(hold=True)

                def extra_f2(tt, mp, do_next=do_next, nxb=nxb):
                    if do_next and tt == 0:
                        for f_ in (2 * mp, 2 * mp + 1):
                            norm_acc(nxb, f_, pn_next, f_ == 0, f_ == 7)

                def after_f2(tt, xb=xb, l=l, last_layer=last_layer, do_next=do_next, nxb=nxb, pn=pn):
                    if not last_layer:
                        norm_finish_tt(xb, pn, tt, (l + 1) * NPL, h_dst)
                        if tt == 1:
                            unhold(pn)
                    elif do_next and tt == 0:
                        norm_finish(nxb, pn_next, 0, h_dst)

                f2_cb = resid_phase(gl, xb, lambda m, tt: 864 + tt * 176 + m * NJ, NJ, ak, akb, pn, False,
                                    after_f2, extra_f2)
                if not last_layer:
                    pending_cb = f2_cb

            pending_final = (xb, pn, s, t0, f2_cb)
        emit_final_act(pending_final)
        emit_final_dve(pending_final)

        P.assign()
        final = [(osem[f], 16 * ngroups) for f in range(8)]
        stats = {}
        with nc.Block() as block:
            @block.tensor
            def _(h):
                stats["pe"] = P.emit_engine("pe", h, sems)

            @block.scalar
            def _(h):
                stats["act"] = P.emit_engine("act", h, sems)

            @block.vector
            def _(h):
                stats["dve"] = P.emit_engine("dve", h, sems)

            @block.gpsimd
            def _(h):
                stats["pool"] = P.emit_engine("pool", h, sems)

            @block.sync
            def _(h):
                stats["sp"] = P.emit_engine("sp", h, sems, final_waits=final)
        build.stats = {e: (len(P.q[e]), stats.get(e)) for e in Prog.ENGS}
    return nc


def prepare_inputs(x, mix_norm_g, w_in, conv_a_w, ln_v_g, ln_v_b, w_s, b_s, w_out,
                   ffn_norm_g, w_up, conv_ffn_w, w_down, final_norm_g, ncores=NCORES):
    x = np.asarray(x, np.float32)
    nb = x.shape[0]
    nseq = nb // ncores
    ws = np.stack([pack_layer(np.asarray(w_in[l], np.float32), np.asarray(w_out[l], np.float32),
                              np.asarray(w_up[l], np.float32), np.asarray(w_down[l], np.float32))
                   for l in range(DEPTH)])
    par = pack_params(np.asarray(mix_norm_g, np.float32), np.asarray(ln_v_g, np.float32),
                      np.asarray(ffn_norm_g, np.float32), np.asarray(conv_a_w, np.float32),
                      np.asarray(conv_ffn_w, np.float32), np.asarray(final_norm_g, np.float32))
    lnb = np.ascontiguousarray(np.asarray(ln_v_b, np.float32))
    bsd = np.ascontiguousarray(np.asarray(b_s, np.float32).reshape(DEPTH, D))
    wst = np.ascontiguousarray(np.asarray(w_s, np.float32).transpose(0, 3, 1, 2))
    xT = np.ascontiguousarray(x.transpose(0, 2, 1))
    in_maps = []
    for c in range(ncores):
        in_maps.append({"xT": xT[c * nseq:(c + 1) * nseq], "ws": ws, "par": par, "lnb": lnb,
                        "bsd": bsd, "wst": wst})
    return nseq, in_maps


def kernel(x, mix_norm_g, w_in, conv_a_w, ln_v_g, ln_v_b, w_s, b_s, w_out,
           ffn_norm_g, w_up, conv_ffn_w, w_down, final_norm_g):
    nseq, in_maps = prepare_inputs(x, mix_norm_g, w_in, conv_a_w, ln_v_g, ln_v_b, w_s, b_s, w_out,
                                   ffn_norm_g, w_up, conv_ffn_w, w_down, final_norm_g)
    nc = build(nseq)
    res = run_bass_kernel_spmd(nc, in_maps, core_ids=list(range(NCORES)))
    outs = [np.asarray(r["oT"]) for r in res.results]
    oT = np.concatenate(outs, axis=0)
    return np.ascontiguousarray(oT.transpose(0, 2, 1)).astype(np.float32)
```

```python
import contextlib
import numpy as np
import concourse.bass as bass
import concourse.mybir as mybir
from concourse.bass_utils import run_bass_kernel_spmd

F32 = mybir.dt.float32
BF16 = mybir.dt.bfloat16
AF = mybir.ActivationFunctionType
ALU = mybir.AluOpType

D = 1024
S = 2048
DEPTH = 2
NCORES = 8
BATCH = 32
G = 1024
KT = 8
DFF = 2816
NJ = 22
EPS = 1e-6
SLOT = 2048
NSLOT_D = 65
NSLOT_L = 76
LW = NSLOT_D * SLOT
NS = 8
NPL = 180
NPAR = DEPTH * NPL + 8
PE_PAD = 0

C_B, C_C, C_XI, C_U, C_V, C_GA, C_GB = 0, 1024, 2048, 3072, 4096, 5120, 6144


def _tiles(w, cols):
    k = w.shape[0] // 128
    a = w.reshape(k, 128, w.shape[1]).transpose(1, 0, 2)
    return a[:, :, cols]


def pack_layer(w_in, w_out, w_up, w_down):
    parts = []
    a_in = w_in.reshape(KT, 128, -1).transpose(1, 0, 2)
    parts.append(a_in[:, :, C_V:C_V + 1024].reshape(128, -1))
    for f in range(8):
        for c0 in (C_XI, C_C, C_B, C_GA, C_U, C_GB):
            parts.append(a_in[:, :, c0 + f * 128:c0 + (f + 1) * 128].reshape(128, -1))
    a_o = w_out.reshape(KT, 128, -1).transpose(1, 0, 2)
    for m in range(8):
        parts.append(a_o[:, :, m * 128:(m + 1) * 128].reshape(128, -1))
    a_u = w_up.reshape(KT, 128, -1).transpose(1, 0, 2)
    for j in range(NJ):
        parts.append(a_u[:, :, j * 128:(j + 1) * 128].reshape(128, -1))
        parts.append(a_u[:, :, DFF + j * 128:DFF + (j + 1) * 128].reshape(128, -1))
    a_d = w_down.reshape(NJ, 128, -1).transpose(1, 0, 2)
    for m in range(8):
        parts.append(a_d[:, :, m * 128:(m + 1) * 128].reshape(128, -1))
    out = np.concatenate(parts, axis=1)
    assert out.shape == (128, LW), out.shape
    return out


def pack_params(mix_norm_g, ln_v_g, ffn_norm_g, conv_a_w, conv_ffn_w, final_norm_g):
    par = np.zeros((128, NPAR), np.float32)
    for l in range(DEPTH):
        b = l * NPL
        par[:, b + 0:b + 8] = mix_norm_g[l].reshape(8, 128).T
        par[:, b + 8:b + 16] = ln_v_g[l].reshape(8, 128).T
        par[:, b + 16:b + 24] = ffn_norm_g[l].reshape(8, 128).T
        par[:, b + 24:b + 48] = conv_a_w[l].reshape(3, 8, 128).transpose(2, 0, 1).reshape(128, 24)
        par[:, b + 48:b + 180] = conv_ffn_w[l].reshape(3, 44, 128).transpose(2, 0, 1).reshape(128, 132)
    par[:, DEPTH * NPL:DEPTH * NPL + 8] = final_norm_g.reshape(8, 128).T
    return par


class Op:
    __slots__ = ("eng", "fn", "deps", "ref", "count", "dma", "dma_sem", "dma_val")

    def __init__(self, eng, fn):
        self.eng = eng
        self.fn = fn
        self.deps = []
        self.ref = False
        self.count = 0
        self.dma = False
        self.dma_sem = None
        self.dma_val = 0


class Buf:
    __slots__ = ("w", "r")

    def __init__(self):
        self.w = None
        self.r = {}


class Prog:
    ENGS = ("pe", "act", "dve", "pool", "sp")

    def __init__(self):
        self.q = {e: [] for e in self.ENGS}
        self.dma_cnt = {}
        self.nuid = 0

    def add(self, eng, fn, reads=(), writes=(), dma_sem=None):
        op = Op(eng, fn)
        is_dma = dma_sem is not None
        deps = {}

        def dep(d, raw):
            if d is None:
                return
            if (not d.dma) and (not is_dma) and d.eng == eng and not raw:
                return
            deps[id(d)] = d

        for b in reads:
            dep(b.w, True)
        for b in writes:
            dep(b.w, False)
            for r in b.r.values():
                dep(r, False)
        op.deps = list(deps.values())
        for d in op.deps:
            d.ref = True
        if is_dma:
            op.dma = True
            op.dma_sem = dma_sem
            self.dma_cnt[id(dma_sem)] = self.dma_cnt.get(id(dma_sem), 0) + 16
            op.dma_val = self.dma_cnt[id(dma_sem)]
        for b in reads:
            if is_dma:
                self.nuid += 1
                b.r[("dma", self.nuid)] = op
            else:
                b.r[eng] = op
        for b in writes:
            b.w = op
            b.r = {}
        self.q[eng].append(op)
        return op

    def assign(self):
        for e in self.ENGS:
            cnt = 0
            for op in self.q[e]:
                if op.ref and not op.dma:
                    cnt += 1
                    op.count = cnt

    def emit_engine(self, eng, h, sems, final_waits=()):
        waited = {}
        nwait = 0
        for op in self.q[eng]:
            for d in op.deps:
                if d.dma:
                    sem, val = d.dma_sem, d.dma_val
                else:
                    sem, val = sems[d.eng], d.count
                if waited.get(id(sem), 0) < val:
                    h.wait_ge(sem, val)
                    waited[id(sem)] = val
                    nwait += 1
            ins = op.fn(h)
            if op.dma:
                ins.then_inc(op.dma_sem, 16)
            elif op.ref:
                ins.then_inc(sems[eng], 1)
        for sem, val in final_waits:
            h.wait_ge(sem, val)
        return nwait


def build(nseq):
    nc = bass.Bass("TRN2", target_bir_lowering=False)
    xT = nc.dram_tensor("xT", [nseq, D, S], F32, kind="ExternalInput").ap()
    ws = nc.dram_tensor("ws", [DEPTH, 128, LW], F32, kind="ExternalInput").ap()
    par_d = nc.dram_tensor("par", [128, NPAR], F32, kind="ExternalInput").ap()
    lnb_d = nc.dram_tensor("lnb", [DEPTH, D], F32, kind="ExternalInput").ap()
    bs_d = nc.dram_tensor("bsd", [DEPTH, D], F32, kind="ExternalInput").ap()
    wst_d = nc.dram_tensor("wst", [DEPTH, 128, 8, 128], F32, kind="ExternalInput").ap()
    oT = nc.dram_tensor("oT", [nseq, D, S], F32, kind="ExternalOutput").ap()

    P = Prog()
    ngroups = 2 * nseq

    with contextlib.ExitStack() as es:
        def sb(name, shape, dt):
            return es.enter_context(nc.sbuf_tensor(name, shape, dt))

        xs = sb("xs", [128, 2, 8, G], F32)
        hs = sb("hs", [128, 8, G], BF16)
        big = sb("big", [128, NJ * G], BF16)
        ring = sb("ring", [128, NS, SLOT], BF16)
        tmp = sb("tmp", [128, 6, G + 2], F32)
        sq = sb("sq", [128, 2, G], BF16)
        rstd = sb("rstd", [128, G], F32)
        par = sb("par_sb", [128, NPAR], F32)
        WT = sb("WT", [128, DEPTH, 8, 128], BF16)
        bias = sb("bias", [128, DEPTH, 8 * 128], F32)
        ones = sb("ones", [128, 128], BF16)
        epsb = sb("epsb", [128, 1], F32)
        hm = sb("hm", [128, DEPTH, 8, 2], F32)
        hf_ = sb("hf", [128, DEPTH, 44, 2], F32)
        st = sb("st", [128, 2, 2, 6], F32)
        mv = sb("mv", [128, 2, 8], F32)
        dmy = sb("dmy", [128, 1], F32)
        bb = sb("bb", [128, 44, 2], F32)
        btt = sb("btt", [128, 44], F32)
        ps = es.enter_context(nc.psum_tensor("ps", [128, 4, G], F32))

        sems = {e: es.enter_context(nc.semaphore("s_" + e)) for e in Prog.ENGS}
        rsem = [es.enter_context(nc.semaphore("rs%d" % i)) for i in range(NS)]
        xsem = [[es.enter_context(nc.semaphore("xs%d_%d" % (b_, i))) for i in range(8)] for b_ in range(2)]
        osem = [es.enter_context(nc.semaphore("os%d" % i)) for i in range(8)]
        csem = es.enter_context(nc.semaphore("cs"))

        XB = [[Buf() for _ in range(8)] for _ in range(2)]
        HB = [Buf() for _ in range(8)]
        AB = [Buf() for _ in range(NJ)]
        RB = [Buf() for _ in range(NS)]
        TB = [Buf() for _ in range(6)]
        SQB = [Buf(), Buf()]
        RSB = Buf()
        PSB = [Buf() for _ in range(4)]
        CONST = Buf()
        HMB = [[Buf() for _ in range(8)] for _ in range(DEPTH)]
        HFB = [Buf() for _ in range(DEPTH)]
        BBB = Buf()
        STB = [Buf(), Buf()]
        DMYB = Buf()

        def act_bf(j):
            return big[:, j * G:(j + 1) * G]

        def out_f32(f):
            return big[:, f * 2 * G:(f + 1) * 2 * G].bitcast(F32)

        state = {"next": 0, "held": set()}

        def alloc_slot(hold=False):
            while True:
                s_ = state["next"] % 4
                state["next"] += 1
                if s_ not in state["held"]:
                    break
            if hold:
                state["held"].add(s_)
            return s_

        def unhold(s_):
            state["held"].discard(s_)

        total_slots = ngroups * DEPTH * NSLOT_L
        wstate = {"issued": 0, "retired": -1}

        def issue_dma():
            q = wstate["issued"]
            lay = (q // NSLOT_L) % DEPTH
            ql = q % NSLOT_L
            if ql >= NSLOT_D:
                ql = ql - NSLOT_D + 54
            r = q % NS
            P.add("pool",
                  lambda e, r=r, lay=lay, ql=ql: e.dma_start(
                      out=ring[:, r, :], in_=ws[lay, :, ql * SLOT:(ql + 1) * SLOT]),
                  writes=[RB[r]], dma_sem=rsem[r])
            wstate["issued"] += 1

        def retire_upto(gq):
            while wstate["retired"] < gq:
                wstate["retired"] += 1
                if wstate["issued"] < total_slots:
                    issue_dma()

        def wtile(gl, ti, ntiles=1, pinned=False):
            gq = gl * NSLOT_L + ti // 16
            if not pinned:
                retire_upto(gq - 1)
            assert wstate["issued"] > gq, (wstate, gq)
            r = gq % NS
            off = (ti % 16) * 128
            return ring[:, r, off:off + 128 * ntiles], RB[r]

        P.add("sp", lambda e: e.dma_start(out=par[:], in_=par_d), writes=[CONST], dma_sem=csem)
        for l in range(DEPTH):
            P.add("sp", lambda e, l=l: e.dma_start(out=tmp[:, l, 0:D], in_=lnb_d[l].partition_broadcast(128)),
                  writes=[TB[l]], dma_sem=csem)
            P.add("sp", lambda e, l=l: e.dma_start(out=tmp[:, 2 + l, 0:D], in_=bs_d[l].partition_broadcast(128)),
                  writes=[TB[2 + l]], dma_sem=csem)
            P.add("sp", lambda e, l=l: e.dma_start(
                out=tmp[:, 4 + l, 0:D].rearrange("p (g t) -> p g t", g=8), in_=wst_d[l]),
                writes=[TB[4 + l]], dma_sem=csem)
        for op in P.q["sp"]:
            if op.dma and op.dma_sem is csem:
                op.dma_val = P.dma_cnt[id(csem)]
        betabf = rstd[:].bitcast(BF16).rearrange("p (l d) -> p l d", l=DEPTH)
        for _ in range(min(NS, total_slots)):
            issue_dma()
        P.add("dve", lambda e: e.memset(ones[:], 1.0), writes=[CONST])
        P.add("dve", lambda e: e.memset(epsb[:], EPS), writes=[CONST])
        for l in range(DEPTH):
            P.add("dve", lambda e, l=l: e.tensor_copy(out=betabf[:, l, :], in_=tmp[:, l, 0:D]),
                  reads=[TB[l]], writes=[CONST, RSB])
            P.add("pool", lambda e, l=l: e.affine_select(
                out=WT[:, l, :, :], in_=tmp[:, 4 + l, 0:D].rearrange("p (g t) -> p g t", g=8),
                pattern=[[0, 8], [1, 128]], compare_op=ALU.is_ge, fill=0.0, base=0,
                channel_multiplier=-1), reads=[TB[4 + l]], writes=[CONST])
        for l in range(DEPTH):
            s_ = alloc_slot()
            for g in range(8):
                P.add("pe", lambda e, l=l, g=g, s_=s_: e.matmul(
                    ps[:, s_, g * 128:(g + 1) * 128], betabf[:, l, g * 128:(g + 1) * 128],
                    WT[:, l, g, :], start=True, stop=True, skip_group_check=True),
                    reads=[CONST, RSB], writes=[PSB[s_]])
            P.add("dve", lambda e, l=l, s_=s_: e.tensor_tensor(
                out=bias[:, l, :], in0=ps[:, s_, :], in1=tmp[:, 2 + l, 0:D], op=ALU.add),
                reads=[PSB[s_], TB[2 + l]], writes=[CONST])

        sqv = sq[:].rearrange("p a g -> p (a g)").rearrange("p (r c) -> p r c", r=4)
        SQB4 = [Buf() for _ in range(4)]
        RSB2 = [RSB, Buf()]
        PNB = [[Buf(), Buf()] for _ in range(4)]
        HB2 = [[Buf(), Buf()] for _ in range(8)]
        sqrot = {"i": 0}

        def H(tt):
            return slice(tt * 512, (tt + 1) * 512)

        def norm_acc_tt(xb, f, tt, pn, first, last, defer=None):
            r = sqrot["i"] % 4
            sqrot["i"] += 1
            P.add("act", lambda e, f=f, r=r, xb=xb, tt=tt: e.activation(
                out=sqv[:, r, :], in_=xs[:, xb, f, H(tt)], func=AF.Square),
                reads=[XB[xb][f]], writes=[SQB4[r]])

            def pe_part(r=r, tt=tt, pn=pn, first=first, last=last):
                P.add("pe", lambda e: e.matmul(
                    ps[:, pn, H(tt)], ones[:], sqv[:, r, :], start=first, stop=last),
                    reads=[SQB4[r], CONST], writes=([PSB[pn]] if first else []) + [PNB[pn][tt]])
            if defer is None:
                pe_part()
            else:
                defer.append(pe_part)

        def norm_acc(xb, f, pn, first, last):
            for tt in range(2):
                norm_acc_tt(xb, f, tt, pn, first, last)

        def norm_rstd_tt(pn, tt):
            P.add("act", lambda e, pn=pn, tt=tt: e.activation(
                out=rstd[:, H(tt)], in_=ps[:, pn, H(tt)], func=AF.Ln, bias=epsb[:, 0:1], scale=1.0 / D),
                reads=[PNB[pn][tt], PSB[pn], CONST], writes=[RSB2[tt]])
            P.add("act", lambda e, tt=tt: e.activation(
                out=rstd[:, H(tt)], in_=rstd[:, H(tt)], func=AF.Exp, scale=-0.5),
                reads=[RSB2[tt]], writes=[RSB2[tt]])

        def norm_apply_tt(xb, tt, gcol, dst):
            for k in range(8):
                out_ap, wb = dst(k, tt)
                P.add("dve", lambda e, k=k, out_ap=out_ap, xb=xb, tt=tt: e.scalar_tensor_tensor(
                    out=out_ap, in0=xs[:, xb, k, H(tt)], scalar=par[:, gcol + k:gcol + k + 1], in1=rstd[:, H(tt)],
                    op0=ALU.mult, op1=ALU.mult),
                    reads=[XB[xb][k], RSB2[tt], CONST], writes=wb)

        def norm_finish_tt(xb, pn, tt, gcol, dst):
            norm_rstd_tt(pn, tt)
            norm_apply_tt(xb, tt, gcol, dst)

        def norm_finish(xb, pn, gcol, dst):
            for tt in range(2):
                norm_finish_tt(xb, pn, tt, gcol, dst)
            unhold(pn)

        def h_dst(k, tt):
            return hs[:, k, H(tt)], [HB2[k][tt]]

        def load_x(g_):
            s_, h_ = divmod(g_, 2)
            b_ = g_ % 2
            for f in range(8):
                P.add("sp", lambda e, f=f, s_=s_, h_=h_, b_=b_: e.dma_start(
                    out=xs[:, b_, f, :], in_=xT[s_, f * 128:(f + 1) * 128, h_ * G:(h_ + 1) * G]),
                    writes=[XB[b_][f]], dma_sem=xsem[b_][f])

        def proj(gl, ti0, rhs_of, rhs_bufs_of, nk, s_, tts=(0, 1), pinned=False, obank=None):
            for k in range(nk):
                w_ap, wb = wtile(gl, ti0 + k, pinned=pinned)
                for tt in tts:
                    ob = tt if obank is None else obank
                    P.add("pe", lambda e, w_ap=w_ap, k=k, tt=tt, s_=s_, ob=ob: e.matmul(
                        ps[:, s_, H(ob)], w_ap, rhs_of(k, tt), start=(k == 0), stop=(k == nk - 1)),
                        reads=[wb] + rhs_bufs_of(k, tt), writes=[PSB[s_]])

        def hk(k, tt):
            return hs[:, k, H(tt)]

        def hkb(k, tt):
            return [HB2[k][tt]]

        def mk(k, tt):
            return act_bf(8 + k)[:, H(tt)]

        def mkb(k, tt):
            return [AB[8 + k]]

        def ak(k, tt):
            return act_bf(k)[:, H(tt)]

        def akb(k, tt):
            return [AB[k]]

        def resid_phase(gl, xb, ti_of, nk, rhs_of, rhs_bufs_of, pn, pinned, after_tt, extra_after_prod,
                        first_kouter_retire=None):
            P.add("act", lambda e: e.activation(out=dmy[:], in_=epsb[:], func=AF.Exp), reads=[CONST], writes=[DMYB])
            deferred = {0: [], 1: []}

            def finish(tt):
                for fn in deferred[tt]:
                    fn()
                deferred[tt] = []
                after_tt(tt)

            for tt in range(2):
                pend = []
                for mp in range(4):
                    s_o = alloc_slot()
                    kouter = pinned or (first_kouter_retire is not None and tt == 0 and mp == 0)
                    if kouter:
                        for k in range(nk):
                            for mi in range(2):
                                w_ap, wb = wtile(gl, ti_of(2 * mp + mi, tt) + k, pinned=True)
                                P.add("pe", lambda e, w_ap=w_ap, k=k, tt=tt, s_=s_o, mi=mi: e.matmul(
                                    ps[:, s_, H(mi)], w_ap, rhs_of(k, tt), start=(k == 0), stop=(k == nk - 1)),
                                    reads=[wb] + rhs_bufs_of(k, tt), writes=[PSB[s_o]])
                        if not pinned:
                            retire_upto(first_kouter_retire)
                    else:
                        for mi in range(2):
                            m = 2 * mp + mi
                            proj(gl, ti_of(m, tt), rhs_of, rhs_bufs_of, nk, s_o, tts=(tt,), pinned=pinned, obank=mi)
                    P.add("dve", lambda e, mp=mp, tt=tt, s_=s_o, xb=xb: e.tensor_tensor(
                        out=xs[:, xb, 2 * mp:2 * mp + 2, H(tt)],
                        in0=ps[:, s_, :].rearrange("p (a c) -> p a c", a=2),
                        in1=xs[:, xb, 2 * mp:2 * mp + 2, H(tt)], op=ALU.add),
                        reads=[PSB[s_o], XB[xb][2 * mp], XB[xb][2 * mp + 1]],
                        writes=[XB[xb][2 * mp], XB[xb][2 * mp + 1]])
                    if tt == 1 and mp == 0:
                        finish(0)
                    extra_after_prod(tt, mp)
                    pend.append(mp)
                    if len(pend) > 1:
                        q_ = pend.pop(0)
                        for m in (2 * q_, 2 * q_ + 1):
                            norm_acc_tt(xb, m, tt, pn, m == 0, False)
                for q_ in pend:
                    for m in (2 * q_, 2 * q_ + 1):
                        norm_acc_tt(xb, m, tt, pn, m == 0, m == 7, defer=deferred[tt])
            return lambda: finish(1)

        def emit_final_act(fin):
            xb_, pn_, s_, t0_, cb_ = fin
            cb_()
            for tt in range(2):
                norm_rstd_tt(pn_, tt)
            unhold(pn_)

        def emit_final_dve(fin):
            xb_, pn_, s_, t0_, cb_ = fin

            def out_dst(k, tt, xb_=xb_):
                return xs[:, xb_, k, H(tt)], [XB[xb_][k]]
            for tt in range(2):
                norm_apply_tt(xb_, tt, DEPTH * NPL, out_dst)
            for f in range(8):
                P.add("sp", lambda e, f=f, s_=s_, t0_=t0_, xb_=xb_: e.dma_start(
                    out=oT[s_, f * 128:(f + 1) * 128, t0_:t0_ + G], in_=xs[:, xb_, f, :]),
                    reads=[XB[xb_][f]], dma_sem=osem[f])

        load_x(0)
        pn0 = alloc_slot(hold=True)
        for f in range(8):
            norm_acc(0, f, pn0, f == 0, f == 7)
        norm_finish(0, pn0, 0, h_dst)
        pending_final = None
        pending_cb = None
        for gi in range(ngroups):
            s, hfi = divmod(gi, 2)
            t0 = hfi * G
            xb = gi % 2

            for l in range(DEPTH):
                gl = gi * DEPTH + l
                pb = l * NPL
                retire_upto(gl * NSLOT_L - 1)
                for n in range(8):
                    s_ = alloc_slot()
                    for k in range(8):
                        for half in range(2):
                            w_ap, wb = wtile(gl, k * 8 + half * 4, 4, pinned=True)
                            P.add("pe", lambda e, w_ap=w_ap, k=k, n=n, half=half, s_=s_: e.matmul(
                                ps[:, s_, H(half)], hs[:, k, n * 128:(n + 1) * 128], w_ap,
                                start=(k == 0), stop=(k == 7)),
                                reads=[wb, HB2[k][n // 4]], writes=[PSB[s_]])
                    if n == 0:
                        if pending_cb is not None:
                            pending_cb()
                            pending_cb = None
                        if l == 0 and pending_final is not None:
                            emit_final_act(pending_final)
                    r = n % 2
                    for half in range(2):
                        P.add("dve", lambda e, r=r, half=half, s_=s_: e.bn_stats(
                            out=st[:, r, half, :], in_=ps[:, s_, H(half)]),
                            reads=[PSB[s_]], writes=[STB[r]])
                    P.add("dve", lambda e, r=r: e.bn_aggr(
                        out=mv[:, r, 0:2], in_=st[:, r, :, :].rearrange("p a b -> p (a b)")),
                        reads=[STB[r]], writes=[STB[r]])
                    P.add("act", lambda e, r=r: e.activation(out=mv[:, r, 2:3], in_=mv[:, r, 1:2], func=AF.Ln,
                                                            bias=epsb[:, 0:1], scale=1.0),
                          reads=[STB[r], CONST], writes=[STB[r]])
                    P.add("act", lambda e, r=r: e.activation(out=mv[:, r, 3:4], in_=mv[:, r, 2:3], func=AF.Exp,
                                                            scale=-0.5),
                          reads=[STB[r]], writes=[STB[r]])
                    P.add("dve", lambda e, r=r: e.tensor_scalar(
                        out=mv[:, r, 4:5], in0=mv[:, r, 0:1], scalar1=-1.0, scalar2=mv[:, r, 3:4],
                        op0=ALU.mult, op1=ALU.mult), reads=[STB[r]], writes=[STB[r]])
                    P.add("act", lambda e, r=r, n=n, s_=s_: e.activation(
                        out=act_bf(n), in_=ps[:, s_, :], func=AF.Identity,
                        bias=mv[:, r, 4:5], scale=mv[:, r, 3:4]),
                        reads=[PSB[s_], STB[r]], writes=[AB[n]])

                if l == 0 and pending_final is not None:
                    emit_final_dve(pending_final)
                    pending_final = None
                if l == 0 and gi + 1 < ngroups:
                    load_x(gi + 1)

                XI, SA, SG = 0, 4, 4
                for f in range(8):
                    cxi = 1
                    tai = 2 + (f % 2)
                    m1i = 5
                    tib = 64 + f * 48
                    cw = [par[:, pb + 24 + j * 8 + f:pb + 24 + j * 8 + f + 1] for j in range(3)]
                    lng = par[:, pb + 8 + f:pb + 8 + f + 1]
                    s_xi = alloc_slot()
                    proj(gl, tib + 0, hk, hkb, 8, s_xi)
                    P.add("act", lambda e, s_=s_xi: e.activation(out=tmp[:, XI, 0:G], in_=ps[:, s_, :], func=AF.Copy),
                          reads=[PSB[s_xi]], writes=[TB[XI]])
                    s_c = alloc_slot()
                    proj(gl, tib + 8, hk, hkb, 8, s_c)
                    if hfi == 0:
                        P.add("dve", lambda e, cxi=cxi: e.memset(tmp[:, cxi, 0:2], 0.0), writes=[TB[cxi]])
                    else:
                        P.add("dve", lambda e, cxi=cxi, l=l, f=f: e.tensor_copy(out=tmp[:, cxi, 0:2], in_=hm[:, l, f, :]),
                              reads=[HMB[l][f]], writes=[TB[cxi]])
                    P.add("dve", lambda e, cxi=cxi, s_=s_c: e.tensor_tensor(
                        out=tmp[:, cxi, 2:G + 2], in0=ps[:, s_, :], in1=tmp[:, XI, 0:G], op=ALU.mult),
                        reads=[PSB[s_c], TB[XI]], writes=[TB[cxi]])
                    if hfi == 0:
                        P.add("dve", lambda e, cxi=cxi, l=l, f=f: e.tensor_copy(out=hm[:, l, f, :], in_=tmp[:, cxi, G:G + 2]),
                              reads=[TB[cxi]], writes=[HMB[l][f]])
                    P.add("act", lambda e, cxi=cxi, tai=tai, c2=cw[2]: e.activation(
                        out=tmp[:, tai, 0:G], in_=tmp[:, cxi, 2:G + 2], func=AF.Copy, scale=c2),
                        reads=[TB[cxi], CONST], writes=[TB[tai]])
                    P.add("dve", lambda e, cxi=cxi, tai=tai, c1=cw[1]: e.scalar_tensor_tensor(
                        out=tmp[:, tai, 0:G], in0=tmp[:, cxi, 1:G + 1], scalar=c1, in1=tmp[:, tai, 0:G],
                        op0=ALU.mult, op1=ALU.add), reads=[TB[cxi], TB[tai], CONST], writes=[TB[tai]])
                    P.add("dve", lambda e, cxi=cxi, tai=tai, c0=cw[0]: e.scalar_tensor_tensor(
                        out=tmp[:, tai, 0:G], in0=tmp[:, cxi, 0:G], scalar=c0, in1=tmp[:, tai, 0:G],
                        op0=ALU.mult, op1=ALU.add), reads=[TB[cxi], TB[tai], CONST], writes=[TB[tai]])
                    s_b = alloc_slot()
                    proj(gl, tib + 16, hk, hkb, 8, s_b)
                    P.add("dve", lambda e, tai=tai, s_=s_b: e.tensor_tensor(
                        out=tmp[:, tai, 0:G], in0=ps[:, s_, :], in1=tmp[:, tai, 0:G], op=ALU.mult),
                        reads=[PSB[s_b], TB[tai]], writes=[TB[tai]])
                    s_ga = alloc_slot()
                    proj(gl, tib + 24, hk, hkb, 8, s_ga)
                    P.add("act", lambda e, s_=s_ga: e.activation(out=tmp[:, SA, 0:G], in_=ps[:, s_, :], func=AF.Sigmoid),
                          reads=[PSB[s_ga]], writes=[TB[SA]])
                    P.add("dve", lambda e, tai=tai: e.tensor_tensor(
                        out=tmp[:, tai, 0:G], in0=tmp[:, tai, 0:G], in1=tmp[:, SA, 0:G], op=ALU.mult),
                        reads=[TB[tai], TB[SA]], writes=[TB[tai]])
                    s_sg = alloc_slot()
                    for n in range(8):
                        P.add("pe", lambda e, n=n, f=f, l=l, s_=s_sg: e.matmul(
                            ps[:, s_, n * 128:(n + 1) * 128], act_bf(n)[:, f * 128:(f + 1) * 128], WT[:, l, f, :],
                            start=True, stop=True, skip_group_check=True),
                            reads=[AB[n], CONST], writes=[PSB[s_sg]])
                    P.add("dve", lambda e, m1i=m1i, lng=lng, l=l, f=f, s_=s_sg: e.scalar_tensor_tensor(
                        out=tmp[:, m1i, 0:G].rearrange("p (n t) -> p n t", n=8),
                        in0=ps[:, s_, :].rearrange("p (n t) -> p n t", n=8), scalar=lng,
                        in1=bias[:, l, f * 128:(f + 1) * 128].unsqueeze(1).broadcast_to([128, 8, 128]),
                        op0=ALU.mult, op1=ALU.add), reads=[PSB[s_sg], CONST], writes=[TB[m1i]])
                    s_u = alloc_slot()
                    proj(gl, tib + 32, hk, hkb, 8, s_u)
                    P.add("dve", lambda e, m1i=m1i, s_=s_u: e.tensor_tensor(
                        out=tmp[:, m1i, 0:G], in0=ps[:, s_, :], in1=tmp[:, m1i, 0:G], op=ALU.mult),
                        reads=[PSB[s_u], TB[m1i]], writes=[TB[m1i]])
                    s_gb = alloc_slot()
                    proj(gl, tib + 40, hk, hkb, 8, s_gb)
                    P.add("act", lambda e, s_=s_gb: e.activation(out=tmp[:, SG, 0:G], in_=ps[:, s_, :], func=AF.Sigmoid),
                          reads=[PSB[s_gb]], writes=[TB[SG]])
                    P.add("dve", lambda e, m1i=m1i: e.tensor_tensor(
                        out=tmp[:, m1i, 0:G], in0=tmp[:, m1i, 0:G], in1=tmp[:, SG, 0:G], op=ALU.mult),
                        reads=[TB[m1i], TB[SG]], writes=[TB[m1i]])
                    P.add("dve", lambda e, m1i=m1i, tai=tai, f=f: e.tensor_tensor(
                        out=act_bf(8 + f), in0=tmp[:, tai, 0:G], in1=tmp[:, m1i, 0:G], op=ALU.add),
                        reads=[TB[tai], TB[m1i]], writes=[AB[8 + f]])

                retire_upto(gl * NSLOT_L + 27)
                pn = alloc_slot(hold=True)

                def after_m3(tt, xb=xb, pb=pb, pn=pn):
                    norm_finish_tt(xb, pn, tt, pb + 16, h_dst)
                    if tt == 1:
                        unhold(pn)

                m3_cb = resid_phase(gl, xb, lambda m, tt: 448 + m * 8, 8, mk, mkb, pn, True, after_m3,
                                    lambda tt, mp: None)
                retire_upto(gl * NSLOT_L + 31)

                if hfi == 1:
                    W0 = par[:, pb + 48:pb + 92]
                    W1 = par[:, pb + 92:pb + 136]
                    P.add("dve", lambda e, l=l, W0=W0: e.tensor_tensor(out=bb[:, :, 1], in0=hf_[:, l, :, 1], in1=W0, op=ALU.mult),
                          reads=[HFB[l], CONST], writes=[BBB])
                    P.add("dve", lambda e, l=l, W0=W0: e.tensor_tensor(out=bb[:, :, 0], in0=hf_[:, l, :, 0], in1=W0, op=ALU.mult),
                          reads=[HFB[l], CONST], writes=[BBB])
                    P.add("dve", lambda e, l=l, W1=W1: e.tensor_tensor(out=btt[:], in0=hf_[:, l, :, 1], in1=W1, op=ALU.mult),
                          reads=[HFB[l], CONST], writes=[BBB])
                    P.add("dve", lambda e: e.tensor_tensor(out=bb[:, :, 0], in0=bb[:, :, 0], in1=btt[:], op=ALU.add),
                          reads=[BBB], writes=[BBB])

                def f1_evac(j, s_g, s_v):
                    cgi = j % 2
                    cvi = 2 + (j % 2)
                    for jj, s_p, ci in ((j, s_g, cgi), (22 + j, s_v, cvi)):
                        fw = [par[:, pb + 48 + t * 44 + jj:pb + 48 + t * 44 + jj + 1] for t in range(3)]
                        P.add("act", lambda e, ci=ci, s_=s_p, w2=fw[2]: e.activation(
                            out=tmp[:, ci, 0:G], in_=ps[:, s_, :], func=AF.Copy, scale=w2),
                            reads=[PSB[s_p], CONST], writes=[TB[ci]])
                        P.add("dve", lambda e, ci=ci, s_=s_p, w1=fw[1]: e.scalar_tensor_tensor(
                            out=tmp[:, ci, 1:G], in0=ps[:, s_, 0:G - 1], scalar=w1, in1=tmp[:, ci, 1:G],
                            op0=ALU.mult, op1=ALU.add), reads=[PSB[s_p], TB[ci], CONST], writes=[TB[ci]])
                        P.add("dve", lambda e, ci=ci, s_=s_p, w0=fw[0]: e.scalar_tensor_tensor(
                            out=tmp[:, ci, 2:G], in0=ps[:, s_, 0:G - 2], scalar=w0, in1=tmp[:, ci, 2:G],
                            op0=ALU.mult, op1=ALU.add), reads=[PSB[s_p], TB[ci], CONST], writes=[TB[ci]])
                        if hfi == 0:
                            P.add("act", lambda e, l=l, jj=jj, s_=s_p: e.activation(
                                out=hf_[:, l, jj, :], in_=ps[:, s_, G - 2:G], func=AF.Copy),
                                reads=[PSB[s_p]], writes=[HFB[l]])
                        else:
                            P.add("dve", lambda e, ci=ci, jj=jj: e.tensor_tensor(
                                out=tmp[:, ci, 0:2], in0=tmp[:, ci, 0:2], in1=bb[:, jj, :], op=ALU.add),
                                reads=[BBB, TB[ci]], writes=[TB[ci]])
                    P.add("act", lambda e, cgi=cgi: e.activation(out=tmp[:, cgi, 0:G], in_=tmp[:, cgi, 0:G], func=AF.Silu),
                          reads=[TB[cgi]], writes=[TB[cgi]])
                    P.add("pool", lambda e, cgi=cgi, cvi=cvi, j=j: e.tensor_tensor(
                        out=act_bf(j), in0=tmp[:, cgi, 0:G], in1=tmp[:, cvi, 0:G], op=ALU.mult),
                        reads=[TB[cgi], TB[cvi]], writes=[AB[j]])

                slots01 = [[None, None], [None, None]]
                for tt in range(2):
                    for j in range(2):
                        for w_ in range(2):
                            if tt == 0:
                                slots01[j][w_] = alloc_slot()
                            proj(gl, 512 + j * 16 + 8 * w_, hk, hkb, 8, slots01[j][w_], tts=(tt,), pinned=True)
                            if tt == 0 and j == 0 and w_ == 0:
                                m3_cb()
                for j in range(2):
                    f1_evac(j, *slots01[j])
                for j in range(2, NJ):
                    s_g = alloc_slot()
                    proj(gl, 512 + j * 16, hk, hkb, 8, s_g)
                    s_v = alloc_slot()
                    proj(gl, 512 + j * 16 + 8, hk, hkb, 8, s_v)
                    f1_evac(j, s_g, s_v)

                pn = alloc_slot(hold=True)
                last_layer = (l == DEPTH - 1)
                do_next = last_layer and (gi + 1 < ngroups)
                nxb = (gi + 1) % 2
                if do_next:
                    pn_next = alloc_slot(hold=True)

                def extra_f2(tt, mp, do_next=do_next, nxb=nxb):
                    if do_next and tt == 0:
                        for f_ in (2 * mp, 2 * mp + 1):
                            norm_acc(nxb, f_, pn_next, f_ == 0, f_ == 7)

                def after_f2(tt, xb=xb, l=l, last_layer=last_layer, do_next=do_next, nxb=nxb, pn=pn):
                    if not last_layer:
                        norm_finish_tt(xb, pn, tt, (l + 1) * NPL, h_dst)
                        if tt == 1:
                            unhold(pn)
                    elif do_next and tt == 0:
                        norm_finish(nxb, pn_next, 0, h_dst)

                retire_upto(gl * NSLOT_L + 53)
                f2_cb = resid_phase(gl, xb, lambda m, tt: 864 + tt * 176 + m * NJ, NJ, ak, akb, pn, False,
                                    after_f2, extra_f2, first_kouter_retire=gl * NSLOT_L + 55)
                if not last_layer:
                    pending_cb = f2_cb

            pending_final = (xb, pn, s, t0, f2_cb)
        emit_final_act(pending_final)
        emit_final_dve(pending_final)

        P.assign()
        final = [(osem[f], 16 * ngroups) for f in range(8)]
        stats = {}
        with nc.Block() as block:
            @block.tensor
            def _(h):
                stats["pe"] = P.emit_engine("pe", h, sems)

            @block.scalar
            def _(h):
                stats["act"] = P.emit_engine("act", h, sems)

            @block.vector
            def _(h):
                stats["dve"] = P.emit_engine("dve", h, sems)

            @block.gpsimd
            def _(h):
                stats["pool"] = P.emit_engine("pool", h, sems)

            @block.sync
            def _(h):
                stats["sp"] = P.emit_engine("sp", h, sems, final_waits=final)
        build.stats = {e: (len(P.q[e]), stats.get(e)) for e in Prog.ENGS}
    return nc


def prepare_inputs(x, mix_norm_g, w_in, conv_a_w, ln_v_g, ln_v_b, w_s, b_s, w_out,
                   ffn_norm_g, w_up, conv_ffn_w, w_down, final_norm_g, ncores=NCORES):
    x = np.asarray(x, np.float32)
    nb = x.shape[0]
    nseq = nb // ncores
    ws = np.stack([pack_layer(np.asarray(w_in[l], np.float32), np.asarray(w_out[l], np.float32),
                              np.asarray(w_up[l], np.float32), np.asarray(w_down[l], np.float32))
                   for l in range(DEPTH)])
    par = pack_params(np.asarray(mix_norm_g, np.float32), np.asarray(ln_v_g, np.float32),
                      np.asarray(ffn_norm_g, np.float32), np.asarray(conv_a_w, np.float32),
                      np.asarray(conv_ffn_w, np.float32), np.asarray(final_norm_g, np.float32))
    lnb = np.ascontiguousarray(np.asarray(ln_v_b, np.float32))
    bsd = np.ascontiguousarray(np.asarray(b_s, np.float32).reshape(DEPTH, D))
    wst = np.ascontiguousarray(np.asarray(w_s, np.float32).transpose(0, 3, 1, 2))
    xT = np.ascontiguousarray(x.transpose(0, 2, 1))
    in_maps = []
    for c in range(ncores):
        in_maps.append({"xT": xT[c * nseq:(c + 1) * nseq], "ws": ws, "par": par, "lnb": lnb,
                        "bsd": bsd, "wst": wst})
    return nseq, in_maps


def kernel(x, mix_norm_g, w_in, conv_a_w, ln_v_g, ln_v_b, w_s, b_s, w_out,
           ffn_norm_g, w_up, conv_ffn_w, w_down, final_norm_g):
    nseq, in_maps = prepare_inputs(x, mix_norm_g, w_in, conv_a_w, ln_v_g, ln_v_b, w_s, b_s, w_out,
                                   ffn_norm_g, w_up, conv_ffn_w, w_down, final_norm_g)
    nc = build(nseq)
    res = run_bass_kernel_spmd(nc, in_maps, core_ids=list(range(NCORES)))
    outs = [np.asarray(r["oT"]) for r in res.results]
    oT = np.concatenate(outs, axis=0)
    return np.ascontiguousarray(oT.transpose(0, 2, 1)).astype(np.float32)
```
